# Optimizing a Trainium2 kernel written in Bass

```python
import math
import jax, jax.numpy as jnp
from jax import lax
import numpy as np

D_MODEL = 2048
BATCH = 4
SEQ = 4096
DEPTH = 2

DIFF_HEADS = 8
DIFF_HEAD_DIM = 64
DIFF_QK_WIDTH = DIFF_HEADS * 2 * DIFF_HEAD_DIM
DIFF_V_WIDTH = DIFF_HEADS * 2 * DIFF_HEAD_DIM
DIFF_Q_BLOCK = 128
MOBA_HEADS = 8
MOBA_HEAD_DIM = 128
MOBA_WIDTH = MOBA_HEADS * MOBA_HEAD_DIM
MOBA_BLOCK = 256
MOBA_TOPK = 3
MOBA_Q_CHUNK = 32
ROPE_THETA = 500000.0
ROT_FRAC_DIV = 4
D_FF = 5632
N_BRANCHES = 2
IN_WIDTH = 2 * DIFF_QK_WIDTH + DIFF_V_WIDTH + 3 * MOBA_WIDTH + N_BRANCHES * D_MODEL
NORM_EPS = 1e-6
SUBLN_EPS = 1e-5
NEG_INF = -1e30

kernel_name = "hybrid_diffattn_moba_macaron"


def rms_norm(x, g, eps=NORM_EPS):
    xf = x.astype(jnp.float32)
    y = xf * lax.rsqrt(jnp.mean(xf * xf, axis=-1, keepdims=True) + eps)
    return (y * g.astype(jnp.float32)).astype(x.dtype)


def rope_tables(seq, head_dim):
    rot = head_dim // ROT_FRAC_DIV
    inv = jnp.power(ROPE_THETA, -jnp.arange(0, rot, 2, dtype=jnp.float32) / rot)
    ang = jnp.arange(seq, dtype=jnp.float32)[:, None] * inv[None, :]
    return jnp.cos(ang), jnp.sin(ang)


def apply_partial_rope(x, cos, sin):
    half = cos.shape[-1]
    rot = 2 * half
    xr, xp = x[..., :rot], x[..., rot:]
    x1, x2 = xr[..., :half], xr[..., half:]
    shape = (1, cos.shape[0]) + (1,) * (x.ndim - 3) + (half,)
    c = cos.reshape(shape).astype(x.dtype)
    s = sin.reshape(shape).astype(x.dtype)
    return jnp.concatenate([x1 * c - x2 * s, x2 * c + x1 * s, xp], axis=-1)


def swiglu(h, w_gu, w_down):
    gu = h @ w_gu
    g, u = gu[..., :D_FF], gu[..., D_FF:]
    return (jax.nn.silu(g) * u) @ w_down


def diff_attention(q, k, v, lq1, lk1, lq2, lk2, subln_g, lambda_init, cos, sin):
    B, S = q.shape[0], q.shape[1]
    q = apply_partial_rope(q, cos, sin)
    k = apply_partial_rope(k, cos, sin)
    f32 = jnp.float32
    lam = (jnp.exp(jnp.sum(lq1.astype(f32) * lk1.astype(f32)))
           - jnp.exp(jnp.sum(lq2.astype(f32) * lk2.astype(f32))) + lambda_init)
    scale = DIFF_HEAD_DIM ** -0.5
    nq = S // DIFF_Q_BLOCK
    qb = q.reshape(B, nq, DIFF_Q_BLOCK, DIFF_HEADS, 2, DIFF_HEAD_DIM).transpose(1, 0, 2, 3, 4, 5)
    starts = jnp.arange(nq, dtype=jnp.int32) * DIFF_Q_BLOCK
    k_pos = jnp.arange(S, dtype=jnp.int32)

    def block(args):
        q_blk, start = args
        s = jnp.einsum('bqhcd,bkhcd->bhcqk', q_blk, k).astype(f32) * scale
        q_pos = start + jnp.arange(DIFF_Q_BLOCK, dtype=jnp.int32)
        s = jnp.where(k_pos[None, :] <= q_pos[:, None], s, NEG_INF)
        p = jax.nn.softmax(s, axis=-1)
        a = p[:, :, 0] - lam * p[:, :, 1]
        return jnp.einsum('bhqk,bkhe->bqhe', a.astype(v.dtype), v)

    o = lax.map(block, (qb, starts))
    o = o.transpose(1, 0, 2, 3, 4).reshape(B, S, DIFF_HEADS, 2 * DIFF_HEAD_DIM)
    o = rms_norm(o, subln_g, SUBLN_EPS) * (1.0 - lambda_init)
    return o.reshape(B, S, DIFF_V_WIDTH)


def moba_attention(q, k, v, cos, sin):
    B, S, H, hd = q.shape
    f32 = jnp.float32
    q = apply_partial_rope(q, cos, sin).transpose(0, 2, 1, 3)
    k = apply_partial_rope(k, cos, sin).transpose(0, 2, 1, 3)
    v = v.transpose(0, 2, 1, 3)
    nb = -(-S // MOBA_BLOCK)
    s_pad = nb * MOBA_BLOCK
    pad = ((0, 0), (0, 0), (0, s_pad - S), (0, 0))
    kp = jnp.pad(k, pad)
    vp = jnp.pad(v, pad)
    kb = kp.reshape(B, H, nb, MOBA_BLOCK, hd)
    vb = vp.reshape(B, H, nb, MOBA_BLOCK, hd)
    k_mean = jnp.mean(kb.astype(f32), axis=3)
    topk = min(MOBA_TOPK, nb)
    scale = hd ** -0.5
    nc = S // MOBA_Q_CHUNK
    qc = q.reshape(B, H, nc, MOBA_Q_CHUNK, hd).transpose(2, 0, 1, 3, 4)
    starts = jnp.arange(nc, dtype=jnp.int32) * MOBA_Q_CHUNK
    blk_ids = jnp.arange(nb, dtype=jnp.int32)
    b_idx = jnp.arange(B)[:, None, None, None]
    h_idx = jnp.arange(H)[None, :, None, None]

    def chunk(args):
        q_c, start = args
        own = start // MOBA_BLOCK
        q_pos = start + jnp.arange(MOBA_Q_CHUNK, dtype=jnp.int32)
        g = jnp.einsum('bhqd,bhnd->bhqn', q_c.astype(f32), k_mean)
        g = jnp.where(blk_ids < own, g, NEG_INF)
        _, sel = lax.top_k(g, topk)
        valid = sel < own
        k_sel = kb[b_idx, h_idx, sel]
        v_sel = vb[b_idx, h_idx, sel]
        s_sel = jnp.einsum('bhqd,bhqrkd->bhqrk', q_c, k_sel).astype(f32) * scale
        s_sel = jnp.where(valid[..., None], s_sel, NEG_INF)
        k_own = lax.dynamic_slice_in_dim(kp, own * MOBA_BLOCK, MOBA_BLOCK, axis=2)
        v_own = lax.dynamic_slice_in_dim(vp, own * MOBA_BLOCK, MOBA_BLOCK, axis=2)
        s_own = jnp.einsum('bhqd,bhkd->bhqk', q_c, k_own).astype(f32) * scale
        own_pos = own * MOBA_BLOCK + jnp.arange(MOBA_BLOCK, dtype=jnp.int32)
        s_own = jnp.where(own_pos[None, :] <= q_pos[:, None], s_own, NEG_INF)
        s_all = jnp.concatenate(
            [s_own, s_sel.reshape(B, H, MOBA_Q_CHUNK, topk * MOBA_BLOCK)], axis=-1)
        p = jax.nn.softmax(s_all, axis=-1).astype(v.dtype)
        p_own = p[..., :MOBA_BLOCK]
        p_sel = p[..., MOBA_BLOCK:].reshape(B, H, MOBA_Q_CHUNK, topk, MOBA_BLOCK)
        return (jnp.einsum('bhqk,bhkd->bhqd', p_own, v_own)
                + jnp.einsum('bhqrk,bhqrkd->bhqd', p_sel, v_sel))

    o = lax.map(chunk, (qc, starts))
    return o.transpose(1, 0, 3, 2, 4).reshape(B, S, H * hd)


def setup_inputs(seed: int = 0) -> dict:
    key = jax.random.key(seed)
    ks = jax.random.split(key, 24)
    f32 = jnp.float32

    def nrm(k, shape, scale):
        return jax.random.normal(k, shape, f32) * scale

    def gain(k, shape):
        return 1.0 + 0.05 * jax.random.normal(k, shape, f32)

    L = DEPTH
    return {
        'x': jax.random.normal(ks[0], (BATCH, SEQ, D_MODEL), f32),
        'ffn1_pre_g': gain(ks[1], (L, D_MODEL)),
        'ffn1_w_gu': nrm(ks[2], (L, D_MODEL, 2 * D_FF), D_MODEL ** -0.5),
        'ffn1_w_down': nrm(ks[3], (L, D_FF, D_MODEL), D_FF ** -0.5),
        'ffn1_post_g': gain(ks[4], (L, D_MODEL)),
        'mix_pre_g': gain(ks[5], (L, D_MODEL)),
        'w_in': nrm(ks[6], (L, D_MODEL, IN_WIDTH), D_MODEL ** -0.5),
        'diff_lq1': nrm(ks[7], (L, DIFF_HEAD_DIM), 0.1),
        'diff_lk1': nrm(ks[8], (L, DIFF_HEAD_DIM), 0.1),
        'diff_lq2': nrm(ks[9], (L, DIFF_HEAD_DIM), 0.1),
        'diff_lk2': nrm(ks[10], (L, DIFF_HEAD_DIM), 0.1),
        'diff_subln_g': gain(ks[11], (L, 2 * DIFF_HEAD_DIM)),
        'w_branch_diff': nrm(ks[12], (L, DIFF_V_WIDTH, D_MODEL), DIFF_V_WIDTH ** -0.5),
        'w_branch_moba': nrm(ks[13], (L, MOBA_WIDTH, D_MODEL), MOBA_WIDTH ** -0.5),
        'w_out': nrm(ks[14], (L, D_MODEL, D_MODEL), D_MODEL ** -0.5),
        'mix_post_g': gain(ks[15], (L, D_MODEL)),
        'ffn2_pre_g': gain(ks[16], (L, D_MODEL)),
        'ffn2_w_gu': nrm(ks[17], (L, D_MODEL, 2 * D_FF), D_MODEL ** -0.5),
        'ffn2_w_down': nrm(ks[18], (L, D_FF, D_MODEL), D_FF ** -0.5),
        'ffn2_post_g': gain(ks[19], (L, D_MODEL)),
    }


def reference(x, ffn1_pre_g, ffn1_w_gu, ffn1_w_down, ffn1_post_g, mix_pre_g, w_in,
              diff_lq1, diff_lk1, diff_lq2, diff_lk2, diff_subln_g,
              w_branch_diff, w_branch_moba, w_out, mix_post_g,
              ffn2_pre_g, ffn2_w_gu, ffn2_w_down, ffn2_post_g):
    B, S, _ = x.shape
    cos_d, sin_d = rope_tables(S, DIFF_HEAD_DIM)
    cos_m, sin_m = rope_tables(S, MOBA_HEAD_DIM)
    o_qa = 0
    o_ka = o_qa + DIFF_QK_WIDTH
    o_va = o_ka + DIFF_QK_WIDTH
    o_qb = o_va + DIFF_V_WIDTH
    o_kb = o_qb + MOBA_WIDTH
    o_vb = o_kb + MOBA_WIDTH
    o_g = o_vb + MOBA_WIDTH
    for l in range(DEPTH):
        lambda_init = 0.8 - 0.6 * math.exp(-0.3 * l)
        h = rms_norm(x, ffn1_pre_g[l])
        x = x + 0.5 * rms_norm(swiglu(h, ffn1_w_gu[l], ffn1_w_down[l]), ffn1_post_g[l])
        h = rms_norm(x, mix_pre_g[l])
        proj = h @ w_in[l]
        qa = proj[..., o_qa:o_ka].reshape(B, S, DIFF_HEADS, 2, DIFF_HEAD_DIM)
        ka = proj[..., o_ka:o_va].reshape(B, S, DIFF_HEADS, 2, DIFF_HEAD_DIM)
        va = proj[..., o_va:o_qb].reshape(B, S, DIFF_HEADS, 2 * DIFF_HEAD_DIM)
        qm = proj[..., o_qb:o_kb].reshape(B, S, MOBA_HEADS, MOBA_HEAD_DIM)
        km = proj[..., o_kb:o_vb].reshape(B, S, MOBA_HEADS, MOBA_HEAD_DIM)
        vm = proj[..., o_vb:o_g].reshape(B, S, MOBA_HEADS, MOBA_HEAD_DIM)
        gate_a = jax.nn.sigmoid(proj[..., o_g:o_g + D_MODEL])
        gate_b = jax.nn.sigmoid(proj[..., o_g + D_MODEL:o_g + 2 * D_MODEL])
        ya = diff_attention(qa, ka, va, diff_lq1[l], diff_lk1[l], diff_lq2[l], diff_lk2[l],
                            diff_subln_g[l], lambda_init, cos_d, sin_d)
        yb = moba_attention(qm, km, vm, cos_m, sin_m)
        merged = gate_a * (ya @ w_branch_diff[l]) + gate_b * (yb @ w_branch_moba[l])
        x = x + rms_norm(merged @ w_out[l], mix_post_g[l])
        h = rms_norm(x, ffn2_pre_g[l])
        x = x + 0.5 * rms_norm(swiglu(h, ffn2_w_gu[l], ffn2_w_down[l]), ffn2_post_g[l])
    return x
```

```python
import math
import numpy as np
import contextlib
import concourse.bass as bass
import concourse.mybir as mybir
from concourse.bass_utils import run_bass_kernel_spmd


F32 = mybir.dt.float32
BF16 = mybir.dt.bfloat16
U8 = mybir.dt.uint8
ALU = mybir.AluOpType
AF = mybir.ActivationFunctionType
AX = mybir.AxisListType
COMPUTE = ('pe', 'act', 'dve', 'pool')
BIGNEG = 30000.0
SEM_CHUNK = 20000


class Buf:
    __slots__ = ('name', 'w', 'r_eng', 'r_dma')

    def __init__(self, name=''):
        self.name = name
        self.w = None
        self.r_eng = {}
        self.r_dma = []


class Op:
    __slots__ = ('eng', 'fn', 'deps', 'dma', 'needed', 'ms', 'sem', 'prev_slot_ms', 'semi')

    def __init__(self, eng, fn, dma):
        self.eng = eng
        self.fn = fn
        self.dma = dma
        self.deps = set()
        self.needed = False
        self.ms = 0
        self.sem = None
        self.semi = 0
        self.prev_slot_ms = 0


class Prog:
    def __init__(self, nc, n_sp_slots=16, n_pool_slots=24):
        self.nc = nc
        self.ops = []
        self.n_slots = {'sp': n_sp_slots, 'pool': n_pool_slots}
        self.last = {}
        self.live_dma = []

    def add(self, eng, fn, reads=(), writes=(), dma=False):
        op = Op(eng, fn, dma)
        deps = op.deps

        def consider(d, raw):
            if d is None or d is op:
                return
            if (not dma) and (not d.dma) and d.eng == eng:
                if eng == 'pe' or not raw:
                    return
            deps.add(d)

        for b in reads:
            consider(b.w, True)
        for b in writes:
            consider(b.w, False)
            for r in b.r_eng.values():
                consider(r, False)
            for r in b.r_dma:
                consider(r, False)
        for b in reads:
            if dma:
                b.r_dma.append(op)
            else:
                b.r_eng[eng] = op
        for b in writes:
            b.w = op
            b.r_eng = {}
            b.r_dma = []
        self.ops.append(op)
        if dma:
            self.live_dma.append(op)
        elif fn is not None:
            self.last[eng] = op
        return op

    def op(self, eng, meth, reads=(), writes=(), **kw):
        return self.add(eng, lambda e: getattr(e, meth)(**kw), reads, writes)

    def mm(self, out, lhsT, rhs, start, stop, reads, writes, skip=False):
        if skip:
            return self.add('pe', lambda e: e.matmul(out, lhsT, rhs, start=start, stop=stop, skip_group_check=True),
                            reads, writes)
        return self.add('pe', lambda e: e.matmul(out, lhsT, rhs, start=start, stop=stop), reads, writes)

    def dma(self, q, out, in_, reads, writes):
        return self.add(q, lambda e: e.dma_start(out=out, in_=in_), reads, writes, dma=True)

    def barrier(self):
        tails = list(self.last.values()) + list(self.live_dma)
        self.live_dma = []
        for eng in ('pe', 'act', 'dve', 'pool', 'sp'):
            o = self.add(eng, None)
            o.deps.update(t for t in tails)

    def emit(self):
        nc = self.nc
        ops = self.ops
        for op in ops:
            for d in op.deps:
                d.needed = True
        with contextlib.ExitStack() as st:
            cnt = {e: 0 for e in COMPUTE}
            for op in ops:
                if (not op.dma) and op.needed and op.fn is not None:
                    cnt[op.eng] += 1
            esem = {e: [st.enter_context(nc.semaphore("s_%s%d" % (e, i)))
                        for i in range(cnt[e] // SEM_CHUNK + 1)] for e in COMPUTE}
            slots = {q: [st.enter_context(nc.semaphore("d_%s%d" % (q, i))) for i in range(n)]
                     for q, n in self.n_slots.items()}
            cnt = {e: 0 for e in COMPUTE}
            slot_uses = {q: [0] * n for q, n in self.n_slots.items()}
            dma_k = {q: 0 for q in self.n_slots}
            for op in ops:
                if op.dma:
                    q = op.eng
                    k = dma_k[q] % self.n_slots[q]
                    dma_k[q] += 1
                    op.prev_slot_ms = 16 * slot_uses[q][k]
                    slot_uses[q][k] += 1
                    op.ms = 16 * slot_uses[q][k]
                    op.sem = slots[q][k]
                elif op.needed and op.fn is not None:
                    c = cnt[op.eng]
                    cnt[op.eng] += 1
                    op.sem = esem[op.eng][c // SEM_CHUNK]
                    op.ms = c % SEM_CHUNK + 1
                    op.semi = c // SEM_CHUNK
            streams = {e: [] for e in ('pe', 'act', 'dve', 'pool', 'sp')}
            for op in ops:
                streams[op.eng].append(op)

            def run_stream(eng_name, e):
                waited = {}

                def wait(sem, val):
                    if val <= 0:
                        return
                    key = id(sem)
                    if waited.get(key, 0) >= val:
                        return
                    waited[key] = val
                    e.wait_ge(sem, val)

                for op in streams[eng_name]:
                    need = {}
                    for d in op.deps:
                        if d.sem is None:
                            continue
                        key = id(d.sem)
                        if key not in need or need[key][1] < d.ms:
                            need[key] = (d.sem, d.ms)
                    for sem, val in need.values():
                        wait(sem, val)
                    if op.dma:
                        wait(op.sem, op.prev_slot_ms)
                        ins = op.fn(e)
                        ins.then_inc(op.sem, 16)
                    else:
                        if op.fn is None:
                            continue
                        ins = op.fn(e)
                        if op.needed:
                            ins.then_inc(op.sem, 1)

            with nc.Block() as block:
                @block.tensor
                def _(e):
                    run_stream('pe', e)

                @block.scalar
                def _(e):
                    run_stream('act', e)

                @block.vector
                def _(e):
                    run_stream('dve', e)

                @block.gpsimd
                def _(e):
                    run_stream('pool', e)

                @block.sync
                def _(e):
                    run_stream('sp', e)
        return nc


_ISZ = {F32: 4, BF16: 2, U8: 1}


class Arena:
    def __init__(self, handle, cap):
        self.h = handle
        self.cap = cap
        self.off = 0

    def alloc(self, shape, dt):
        n = 1
        for s in shape[1:]:
            n *= s
        nb = n * _ISZ[dt]
        nb_al = (nb + 31) // 32 * 32
        assert self.off + nb_al <= self.cap, ("SBUF arena overflow", self.off, nb_al, self.cap)
        v = self.h[:, self.off:self.off + nb].bitcast(dt)
        self.off += nb_al
        if shape[0] < 128:
            v = v[0:shape[0], :]
        if len(shape) == 3:
            v = v.rearrange("p (a b) -> p a b", b=shape[2])
        elif len(shape) == 4:
            v = v.rearrange("p (a b c) -> p a b c", b=shape[2], c=shape[3])
        return v


class Ring:
    def __init__(self, ar, shape, dtype, n):
        self.t = [ar.alloc(shape, dtype) for _ in range(n)]
        self.b = [Buf() for _ in range(n)]
        self.i = 0
        self.n = n

    def next(self):
        k = self.i % self.n
        self.i += 1
        return self.t[k], self.b[k]


def rms_stats(P, env, src, src_bufs, KC, N, eps_ap, rstd, rstd_b, sq_ring, ps, psb):
    for k in range(KC):
        sq_t, sq_b = sq_ring.next()
        eng = ('pool', 'dve')[k % 2]
        P.op(eng, 'tensor_tensor', [src_bufs[k]], [sq_b], out=sq_t[:, :N], in0=src[:, k, :], in1=src[:, k, :], op=ALU.mult)
        P.mm(ps[:, :N], env['ones'][:, :], sq_t[:, :N], k == 0, k == KC - 1, [sq_b, env['const_b']], [psb])
    P.op('act', 'activation', [psb, env['const_b']], [rstd_b], out=rstd[:, :N], in_=ps[:, :N], func=AF.Sqrt, bias=eps_ap)
    P.op('dve', 'reciprocal', [rstd_b], [rstd_b], out=rstd[:, :N], in_=rstd[:, :N])


def post_norm_residual(P, env, x_t, x_b, y_t, y_b, g_ap, KC, TT, rstd, rstd_b, sq_ring, ps, psb, out_v, t0):
    rms_stats(P, env, y_t, y_b, KC, TT, env['eps6'], rstd, rstd_b, sq_ring, ps, psb)
    outs = []
    for k in range(KC):
        P.op('dve', 'scalar_tensor_tensor', [y_b[k], env['const_b'], rstd_b], [y_b[k]],
             out=y_t[:, k, :], in0=y_t[:, k, :], scalar=g_ap[:, k:k + 1], in1=rstd[:, :], op0=ALU.mult, op1=ALU.mult)
        P.op('pool', 'tensor_tensor', [x_b[k], y_b[k]], [x_b[k]], out=x_t[:, k, :], in0=x_t[:, k, :], in1=y_t[:, k, :], op=ALU.add)
        outs.append(P.dma('sp', out_v[:, k, t0:t0 + TT], x_t[:, k, :], [x_b[k]], []))
    return outs


def ffn_phase(P, env, x_in, x_out, h_out, wgu, wd, g1, g2, g3, tiles):
    D, F, TT = env['D'], env['F'], 512
    KC, FC = D // 128, F // 128
    DG, DGW = max(1, D // 512), min(D, 512)
    DCG = DGW // 128
    SL = 11 if FC % 11 == 0 else (4 if FC % 4 == 0 else 1)
    ar = env['arena']
    ar.off = env['arena_mark']
    ps, ps_b = env['ps'], env['ps_b']
    x_t = ar.alloc([128, KC, TT], F32)
    h_t = ar.alloc([128, KC, TT], BF16)
    a_t = ar.alloc([128, FC, TT], BF16)
    y_t = ar.alloc([128, KC, TT], F32)
    rstd = ar.alloc([128, TT], F32)
    x_b = [Buf() for _ in range(KC)]
    h_b = [Buf() for _ in range(KC)]
    a_b = [Buf() for _ in range(FC)]
    y_b = [Buf() for _ in range(KC)]
    rstd_b = Buf()
    wgu_ring = Ring(ar, [128, KC, 256], BF16, 3)
    wd_ring = Ring(ar, [128, SL, DGW], BF16, 2)
    sq_ring = Ring(ar, [128, TT], F32, 3)
    sg_ring = Ring(ar, [128, TT], F32, 2)
    xin_v = x_in.rearrange("(c p) t -> p c t", p=128)
    xout_v = x_out.rearrange("(c p) t -> p c t", p=128)
    wd_v = wd.rearrange("(c p) d -> p c d", p=128)
    outs = []
    for ti in tiles:
        t0 = ti * TT
        for k in range(KC):
            P.dma('sp', x_t[:, k, :], xin_v[:, k, t0:t0 + TT], [], [x_b[k]])
        rms_stats(P, env, x_t, x_b, KC, TT, env['eps6'], rstd, rstd_b, sq_ring, ps[4], ps_b[4])
        for k in range(KC):
            P.op('dve', 'scalar_tensor_tensor', [x_b[k], env['const_b'], rstd_b], [h_b[k]],
                 out=h_t[:, k, :], in0=x_t[:, k, :], scalar=g1[:, k:k + 1], in1=rstd[:, :], op0=ALU.mult, op1=ALU.mult)
        for fc in range(FC):
            w_t, w_b = wgu_ring.next()
            P.dma('pool', w_t[:, :, :], wgu[fc], [], [w_b])
            pg, pgb = ps[(fc % 2) * 2], ps_b[(fc % 2) * 2]
            pu, pub = ps[(fc % 2) * 2 + 1], ps_b[(fc % 2) * 2 + 1]
            for k in range(KC):
                P.mm(pg[:, :TT], w_t[:, k, 0:128], h_t[:, k, :], k == 0, k == KC - 1, [w_b, h_b[k]], [pgb])
            for k in range(KC):
                P.mm(pu[:, :TT], w_t[:, k, 128:256], h_t[:, k, :], k == 0, k == KC - 1, [w_b, h_b[k]], [pub])
            s_t, s_b = sg_ring.next()
            P.op('act', 'activation', [pgb], [s_b], out=s_t[:, :], in_=pg[:, :TT], func=AF.Silu)
            P.op('dve', 'tensor_tensor', [s_b, pub], [a_b[fc]], out=a_t[:, fc, :], in0=s_t[:, :], in1=pu[:, :TT], op=ALU.mult)
        for dg in range(DG):
            for sl in range(FC // SL):
                w_t, w_b = wd_ring.next()
                P.dma('pool', w_t[:, :, :], wd_v[:, sl * SL:(sl + 1) * SL, dg * DGW:(dg + 1) * DGW], [], [w_b])
                for j in range(SL):
                    fc = sl * SL + j
                    for dc in range(DCG):
                        P.mm(ps[4 + dc][:, :TT], w_t[:, j, dc * 128:(dc + 1) * 128], a_t[:, fc, :],
                             fc == 0, fc == FC - 1, [w_b, a_b[fc]], [ps_b[4 + dc]])
            for dc in range(DCG):
                kk = dg * DCG + dc
                P.op('act', 'activation', [ps_b[4 + dc]], [y_b[kk]], out=y_t[:, kk, :], in_=ps[4 + dc][:, :TT], func=AF.Copy)
        outs += post_norm_residual(P, env, x_t, x_b, y_t, y_b, g2, KC, TT, rstd, rstd_b, sq_ring, ps[4], ps_b[4], xout_v, t0)
        if h_out is not None:
            rms_stats(P, env, x_t, x_b, KC, TT, env['eps6'], rstd, rstd_b, sq_ring, ps[4], ps_b[4])
            hv = h_out.rearrange("(c p) t -> p c t", p=128)
            for k in range(KC):
                P.op('dve', 'scalar_tensor_tensor', [x_b[k], env['const_b'], rstd_b], [h_b[k]],
                     out=h_t[:, k, :], in0=x_t[:, k, :], scalar=g3[:, k:k + 1], in1=rstd[:, :], op0=ALU.mult, op1=ALU.mult)
                outs.append(P.dma('sp', hv[:, k, t0:t0 + TT], h_t[:, k, :], [h_b[k]], []))
    return outs


def merge_phase(P, env, x_in, h_in, yaT, ybT, x_out, wmix, wo, g2, tiles):
    D, E, TT = env['D'], env['E'], 512
    KC, EC = D // 128, E // 128
    NW = 2 * KC + 2 * EC
    ar = env['arena']
    ar.off = env['arena_mark']
    ps, ps_b = env['ps'], env['ps_b']
    x_t = ar.alloc([128, KC, TT], F32)
    h_t = ar.alloc([128, KC, TT], BF16)
    ya_t = ar.alloc([128, EC, TT], BF16)
    yb_t = ar.alloc([128, EC, TT], BF16)
    m_t = ar.alloc([128, KC, TT], BF16)
    y_t = ar.alloc([128, KC, TT], F32)
    rstd = ar.alloc([128, TT], F32)
    x_b = [Buf() for _ in range(KC)]
    y_b = [Buf() for _ in range(KC)]
    m_b = [Buf() for _ in range(KC)]
    h_b, ya_b, yb_b, rstd_b = Buf(), Buf(), Buf(), Buf()
    wm_ring = Ring(ar, [128, NW, 128], BF16, 3)
    wo_ring = Ring(ar, [128, KC, 128], BF16, 3)
    sq_ring = Ring(ar, [128, TT], F32, 3)
    sg_ring = Ring(ar, [128, TT], F32, 4)
    xin_v = x_in.rearrange("(c p) t -> p c t", p=128)
    xout_v = x_out.rearrange("(c p) t -> p c t", p=128)
    h_v = h_in.rearrange("(c p) t -> p c t", p=128)
    ya_v = yaT.rearrange("(c p) t -> p c t", p=128)
    yb_v = ybT.rearrange("(c p) t -> p c t", p=128)
    outs = []
    for ti in tiles:
        t0 = ti * TT
        for k in range(KC):
            P.dma('sp', x_t[:, k, :], xin_v[:, k, t0:t0 + TT], [], [x_b[k]])
        P.dma('sp', h_t[:, :, :], h_v[:, :, t0:t0 + TT], [], [h_b])
        P.dma('sp', ya_t[:, :, :], ya_v[:, :, t0:t0 + TT], [], [ya_b])
        P.dma('sp', yb_t[:, :, :], yb_v[:, :, t0:t0 + TT], [], [yb_b])
        for dc in range(KC):
            w_t, w_b = wm_ring.next()
            P.dma('pool', w_t[:, :, :], wmix[dc], [], [w_b])
            base = (dc % 2) * 3
            pga, pgab = ps[base], ps_b[base]
            pgb, pgbb = ps[base + 1], ps_b[base + 1]
            pab, pabb = ps[base + 2], ps_b[base + 2]
            for k in range(KC):
                P.mm(pga[:, :], w_t[:, k, :], h_t[:, k, :], k == 0, k == KC - 1, [w_b, h_b], [pgab])
            for k in range(KC):
                P.mm(pgb[:, :], w_t[:, KC + k, :], h_t[:, k, :], k == 0, k == KC - 1, [w_b, h_b], [pgbb])
            sa, sab = sg_ring.next()
            sbt, sbb = sg_ring.next()
            P.op('act', 'activation', [pgab], [sab], out=sa[:, :], in_=pga[:, :], func=AF.Sigmoid)
            P.op('act', 'activation', [pgbb], [sbb], out=sbt[:, :], in_=pgb[:, :], func=AF.Sigmoid)
            for e_ in range(EC):
                P.mm(pab[:, :], w_t[:, 2 * KC + e_, :], ya_t[:, e_, :], e_ == 0, e_ == EC - 1, [w_b, ya_b], [pabb])
            P.op('dve', 'tensor_tensor', [sab, pabb], [sab], out=sa[:, :], in0=sa[:, :], in1=pab[:, :], op=ALU.mult)
            for e_ in range(EC):
                P.mm(pab[:, :], w_t[:, 2 * KC + EC + e_, :], yb_t[:, e_, :], e_ == 0, e_ == EC - 1, [w_b, yb_b], [pabb])
            P.op('dve', 'tensor_tensor', [sbb, pabb], [sbb], out=sbt[:, :], in0=sbt[:, :], in1=pab[:, :], op=ALU.mult)
            P.op('pool', 'tensor_tensor', [sab, sbb], [m_b[dc]], out=m_t[:, dc, :], in0=sa[:, :], in1=sbt[:, :], op=ALU.add)
        for dc in range(KC):
            w_t, w_b = wo_ring.next()
            P.dma('pool', w_t[:, :, :], wo[dc], [], [w_b])
            pz, pzb = ps[6 + dc % 2], ps_b[6 + dc % 2]
            for k in range(KC):
                P.mm(pz[:, :], w_t[:, k, :], m_t[:, k, :], k == 0, k == KC - 1, [w_b, m_b[k]], [pzb])
            P.op('act', 'activation', [pzb], [y_b[dc]], out=y_t[:, dc, :], in_=pz[:, :], func=AF.Copy)
        outs += post_norm_residual(P, env, x_t, x_b, y_t, y_b, g2, KC, TT, rstd, rstd_b, sq_ring, ps[6], ps_b[6], xout_v, t0)
    return outs


def attn_phase(P, env, hT, yaT, ybT, wq, lv_ap, gs_ap, lambda_init, n_diff, n_moba, q_tiles):
    D, S = env['D'], env['S']
    KC = D // 128
    NT, NSUB, NB = S // 512, S // 128, S // 256
    NH = n_diff + n_moba
    ar = env['arena']
    ar.off = env['arena_mark']
    ps, ps_b = env['ps'], env['ps_b']
    const_b = env['const_b']
    ident, tri, oh = env['ident'], env['tri'], env['oh']
    pm, al01, own01 = env['pm'], env['al01'], env['own01']
    rope = env['rope']
    h_ring = Ring(ar, [128, KC, 512], BF16, 2)
    w_ring = Ring(ar, [128, 5, KC, 128], BF16, 2)
    rope_ring = Ring(ar, [128, 2, 512], F32, 2)
    tmp_ring = Ring(ar, [128, 512], F32, 4)
    rot_ring = Ring(ar, [128, 512], F32, 2)
    pt_ring = Ring(ar, [128, 512], BF16, 6)
    ot_ring = Ring(ar, [128, 512], BF16, 2)
    QT = ar.alloc([128, S], BF16)
    KT = ar.alloc([128, S], BF16)
    VA = ar.alloc([128, NSUB, 129], BF16)
    QT_b = [Buf() for _ in range(NT)]
    KT_b = [Buf() for _ in range(NT)]
    VA_b = [Buf() for _ in range(NT)]
    lprod = ar.alloc([128, 2, 64], F32)
    lsum = ar.alloc([128, 2], F32)
    neglam = ar.alloc([128, 1], F32)
    gsc = ar.alloc([128, 128], F32)
    gm = ar.alloc([128, NSUB, NB], F32)
    top8 = ar.alloc([128, NSUB, 8], F32)
    sel = ar.alloc([128, NSUB, NB], F32)
    Bbf = ar.alloc([128, NSUB, NB], BF16)
    BT = ar.alloc([NB, S], BF16)
    kmacc = ar.alloc([128, NB], F32)
    kmb = ar.alloc([128, NB], BF16)
    osb = ar.alloc([128, 4, 128], BF16)
    o1 = ar.alloc([128, 128], F32)
    od = ar.alloc([128, 128], F32)
    junk = ar.alloc([128, 128], F32)
    sm = ar.alloc([128, 8], F32)
    gm_b, top8_b, sel_b, Bbf_b, BT_b, km_b, kmb_b = Buf(), Buf(), Buf(), Buf(), Buf(), Buf(), Buf()
    osb_b, o1_b, od_b, junk_b, sm_b, lam_b = Buf(), Buf(), Buf(), Buf(), Buf(), Buf()
    hT_v = hT.rearrange("(c p) t -> p c t", p=128)
    tp = ps[3][:, :].bitcast(BF16)[:, 0:512]
    tpB = ps[6][:, :].bitcast(BF16)[0:NB, 0:512]

    P.op('pool', 'memset', [], [const_b, VA_b[0]], ap=VA[:, :, 128:129], constant=1.0)
    P.op('dve', 'tensor_tensor', [const_b], [lam_b], out=lprod[:, :, :], in0=lv_ap[:, 0:2, :], in1=lv_ap[:, 2:4, :], op=ALU.mult)
    P.op('dve', 'tensor_reduce', [lam_b], [lam_b], out=lsum[:, :], in_=lprod[:, :, :], axis=AX.X, op=ALU.add)
    P.op('act', 'activation', [lam_b], [lam_b], out=lsum[:, :], in_=lsum[:, :], func=AF.Exp)
    P.op('dve', 'tensor_tensor', [lam_b], [lam_b], out=neglam[:, :], in0=lsum[:, 1:2], in1=lsum[:, 0:1], op=ALU.subtract)
    P.op('dve', 'tensor_scalar', [lam_b], [lam_b], out=neglam[:, :], in0=neglam[:, :], scalar1=float(-lambda_init), scalar2=None, op0=ALU.add)
    P.op('dve', 'tensor_scalar', [const_b], [lam_b], out=gsc[:, :], in0=gs_ap[:, :], scalar1=float(1.0 - lambda_init), scalar2=None, op0=ALU.mult)
    outs = []
    for hd in range(NH):
        is_moba = hd >= n_diff
        nmaps = 1 if is_moba else 2
        kd = 128 if is_moba else 64
        scale = float(kd ** -0.5)
        w_t, w_b = w_ring.next()
        for j in range(5):
            P.dma('pool', w_t[:, j, :, :], wq[hd, j], [], [w_b])
        for ti in range(NT):
            t0 = ti * 512
            h_t, h_b = h_ring.next()
            P.dma('sp', h_t[:, :, :], hT_v[:, :, t0:t0 + 512], [], [h_b])
            r_t, r_b = rope_ring.next()
            P.dma('sp', r_t[:, :, :], rope[1 if is_moba else 0][:, :, t0:t0 + 512], [], [r_b])
            for j in range(4):
                for k in range(KC):
                    P.mm(ps[j][:, :], w_t[:, j, k, :], h_t[:, k, :], k == 0, k == KC - 1, [w_b, h_b], [ps_b[j]])
            for s4 in range(4):
                for k in range(KC):
                    P.mm(ps[4][:, s4 * 128:(s4 + 1) * 128], h_t[:, k, s4 * 128:(s4 + 1) * 128], w_t[:, 4, k, :],
                         k == 0, k == KC - 1, [w_b, h_b], [ps_b[4]])
            for which, dst, dst_b in ((0, QT, QT_b), (1, KT, KT_b)):
                pa, pab = ps[2 * which], ps_b[2 * which]
                pp, ppb = ps[2 * which + 1], ps_b[2 * which + 1]
                ta, tab = tmp_ring.next()
                tb, tbb = tmp_ring.next()
                rt, rtb = rot_ring.next()
                P.op('dve', 'tensor_tensor', [pab, r_b], [tab], out=ta[:, :], in0=pa[:, :], in1=r_t[:, 0, :], op=ALU.mult)
                P.op('dve', 'tensor_tensor', [ppb, r_b], [tbb], out=tb[:, :], in0=pp[:, :], in1=r_t[:, 1, :], op=ALU.mult)
                P.op('pool', 'tensor_tensor', [tab, tbb], [rtb], out=rt[:, :], in0=ta[:, :], in1=tb[:, :], op=ALU.add)
                P.op('act', 'activation', [rtb], [dst_b[ti]], out=dst[:, t0:t0 + 512], in_=rt[:, :], func=AF.Copy)
                if is_moba and which == 1:
                    P.op('dve', 'tensor_reduce', [rtb], [km_b], out=kmacc[:, 2 * ti:2 * ti + 2],
                         in_=rt[:, :].rearrange("p (b k) -> p b k", k=256), axis=AX.X, op=ALU.add)
            P.op('act', 'activation', [ps_b[4], const_b], [VA_b[ti]], out=VA[:, 4 * ti:4 * ti + 4, 0:128],
                 in_=ps[4][:, :].rearrange("p (s c) -> p s c", c=128), func=AF.Copy)
        if is_moba:
            P.op('act', 'activation', [km_b], [kmb_b], out=kmb[:, :], in_=kmacc[:, :], func=AF.Copy, scale=1.0 / 256.0)
            gps = ps[5]
            for s in range(NSUB):
                P.mm(gps[:, s * NB:(s + 1) * NB], QT[:, s * 128:(s + 1) * 128], kmb[:, :], True, True,
                     [QT_b[s // 4], kmb_b], [ps_b[5]])
            P.op('dve', 'tensor_tensor', [ps_b[5], const_b], [gm_b], out=gm[:, :, :],
                 in0=gps[:, 0:NSUB * NB].rearrange("p (s n) -> p s n", n=NB), in1=pm[:, :, :], op=ALU.add)
            for s in range(NSUB):
                P.op('dve', 'max', [gm_b], [top8_b], out=top8[:, s, :], in_=gm[:, s, :])
            for s in range(NSUB):
                P.op('dve', 'tensor_scalar', [gm_b, top8_b], [sel_b], out=sel[:, s, :], in0=gm[:, s, :],
                     scalar1=top8[:, s, 2:3], scalar2=None, op0=ALU.is_ge)
            P.op('dve', 'tensor_tensor', [sel_b, const_b], [sel_b], out=sel[:, :, :], in0=sel[:, :, :], in1=al01[:, :, :], op=ALU.mult)
            P.op('dve', 'tensor_tensor', [sel_b, const_b], [sel_b], out=sel[:, :, :], in0=sel[:, :, :], in1=own01[:, :, :], op=ALU.max)
            P.op('dve', 'tensor_scalar', [sel_b], [Bbf_b], out=Bbf[:, :, :], in0=sel[:, :, :], scalar1=-1.0, scalar2=BIGNEG,
                 op0=ALU.add, op1=ALU.mult)
            for s4 in range(NT):
                for j in range(4):
                    s = s4 * 4 + j
                    P.add('pe', (lambda o, i: lambda e: e.transpose(o, i, ident[:, :]))(tpB[:, j * 128:(j + 1) * 128], Bbf[:, s, :]),
                          [Bbf_b, const_b], [ps_b[6]])
                P.op('act', 'activation', [ps_b[6]], [BT_b], out=BT[:, s4 * 512:(s4 + 1) * 512], in_=tpB[:, :], func=AF.Copy)
        for qt in q_tiles:
            nk = 4 * qt + 4
            obanks = [(ps[4], ps_b[4]), (ps[5], ps_b[5]), (ps[6], ps_b[6]), (ps[7], ps_b[7])]

            def oacc(c, si):
                bk, bb = obanks[c * 2 + si // 2]
                return bk[:, (si % 2) * 129:(si % 2) * 129 + 129], bb

            for kc in range(nk):
                off = max(0, kc - 4 * qt)
                q0 = qt * 512 + off * 128
                N = 512 - off * 128
                for c in range(nmaps):
                    bi = (kc * nmaps + c) % 3 if not is_moba else (kc % 3)
                    sps, spsb = ps[bi], ps_b[bi]
                    if is_moba:
                        P.mm(sps[:, :N], KT[:, kc * 128:(kc + 1) * 128], QT[:, q0:q0 + N], True, False,
                             [KT_b[kc // 4], QT_b[qt]], [spsb])
                        P.mm(sps[:, :N], oh[:, kc // 2, :], BT[:, q0:q0 + N], False, True, [const_b, BT_b], [spsb])
                    else:
                        P.mm(sps[:, :N], KT[c * 64:(c + 1) * 64, kc * 128:(kc + 1) * 128],
                             QT[c * 64:(c + 1) * 64, q0:q0 + N], True, True, [KT_b[kc // 4], QT_b[qt]], [spsb])
                    p_t, p_b = pt_ring.next()
                    P.op('act', 'activation', [spsb], [p_b], out=p_t[:, :N], in_=sps[:, :N], func=AF.Exp, scale=scale)
                    if kc >= 4 * qt:
                        P.op('pool', 'tensor_tensor', [p_b, const_b], [p_b], out=p_t[:, 0:128], in0=p_t[:, 0:128], in1=tri[:, :], op=ALU.mult)
                    for si in range(off, 4):
                        oap, ob = oacc(c, si)
                        P.mm(oap, p_t[:, (si - off) * 128:(si - off + 1) * 128], VA[:, kc, :],
                             (kc == 0 and si % 2 == 0), kc == 4 * qt + si, [p_b, VA_b[kc // 4]], [ob], skip=True)
            for si in range(4):
                o0, ob0 = oacc(0, si)
                if is_moba:
                    P.op('dve', 'reciprocal', [ob0], [sm_b], out=sm[:, 0:1], in_=o0[:, 128:129])
                    P.op('act', 'activation', [ob0, sm_b], [osb_b], out=osb[:, si, :], in_=o0[:, 0:128], func=AF.Copy, scale=sm[:, 0:1])
                else:
                    o1p, ob1 = oacc(1, si)
                    P.op('dve', 'reciprocal', [ob0], [sm_b], out=sm[:, 0:1], in_=o0[:, 128:129])
                    P.op('dve', 'reciprocal', [ob1, sm_b], [sm_b], out=sm[:, 1:2], in_=o1p[:, 128:129])
                    P.op('dve', 'tensor_tensor', [sm_b, lam_b], [sm_b], out=sm[:, 1:2], in0=sm[:, 1:2], in1=neglam[:, :], op=ALU.mult)
                    P.op('act', 'activation', [ob0, sm_b], [o1_b], out=o1[:, :], in_=o0[:, 0:128], func=AF.Copy, scale=sm[:, 0:1])
                    P.op('dve', 'scalar_tensor_tensor', [ob1, sm_b, o1_b], [od_b], out=od[:, :], in0=o1p[:, 0:128], scalar=sm[:, 1:2],
                         in1=o1[:, :], op0=ALU.mult, op1=ALU.add)
                    P.op('pool', 'memset', [], [sm_b], ap=sm[:, 2:3], constant=0.0)
                    P.op('act', 'activation', [od_b, sm_b], [junk_b, sm_b], out=junk[:, :], in_=od[:, :], func=AF.Square, accum_out=sm[:, 2:3])
                    P.op('act', 'activation', [sm_b, const_b], [sm_b], out=sm[:, 3:4], in_=sm[:, 2:3], func=AF.Sqrt, scale=1.0 / 128.0, bias=env['eps5'])
                    P.op('dve', 'reciprocal', [sm_b], [sm_b], out=sm[:, 3:4], in_=sm[:, 3:4])
                    P.op('dve', 'scalar_tensor_tensor', [od_b, sm_b, lam_b], [osb_b], out=osb[:, si, :], in0=od[:, :], scalar=sm[:, 3:4],
                         in1=gsc[:, :], op0=ALU.mult, op1=ALU.mult)
            for si in range(4):
                P.add('pe', (lambda o, i: lambda e: e.transpose(o, i, ident[:, :]))(tp[:, si * 128:(si + 1) * 128], osb[:, si, :]),
                      [osb_b, const_b], [ps_b[3]])
            o_t, o_b = ot_ring.next()
            P.op('act', 'activation', [ps_b[3]], [o_b], out=o_t[:, :], in_=tp[:, :], func=AF.Copy)
            if is_moba:
                dst = ybT[(hd - n_diff) * 128:(hd - n_diff + 1) * 128, qt * 512:(qt + 1) * 512]
            else:
                dst = yaT[hd * 128:(hd + 1) * 128, qt * 512:(qt + 1) * 512]
            outs.append(P.dma('sp', dst, o_t[:, :], [o_b], []))
    return outs


def build_fused(D, F, S, nH, L, lambda_inits, own_from=0):
    KC, FC, E = D // 128, F // 128, nH * 128
    NT, NSUB, NB = S // 512, S // 128, S // 256
    NW = 2 * KC + 2 * (E // 128)
    NH2 = 2 * nH
    nc = bass.Bass("TRN2", target_bir_lowering=False)
    dt = lambda name, shape, dtp, kind: nc.dram_tensor(name, shape, dtp, kind=kind).ap()
    xT = dt("xT", [D, S], F32, "ExternalInput")
    outT = dt("outT", [D, S], F32, "ExternalOutput")
    wgu = [dt("wgu%d" % i, [L, FC, 128, KC, 256], F32, "ExternalInput") for i in (1, 2)]
    wd = [dt("wd%d" % i, [L, F, D], F32, "ExternalInput") for i in (1, 2)]
    wq = dt("wq", [L, NH2, 5, 128, KC, 128], F32, "ExternalInput")
    wmix = dt("wmix", [L, KC, 128, NW, 128], F32, "ExternalInput")
    wo = dt("wo", [L, KC, 128, KC, 128], F32, "ExternalInput")
    gv_d = dt("gv", [128, L * 6, KC], F32, "ExternalInput")
    lv_d = dt("lvec", [128, L * 4, 64], F32, "ExternalInput")
    gs_d = dt("gsub", [128, L, 128], F32, "ExternalInput")
    rope = dt("rope", [2, 128, 2, S], F32, "ExternalInput")
    gconst = dt("gconst", [3, 128, NSUB, NB], F32, "ExternalInput")
    ident_d = dt("ident", [128, 128], F32, "ExternalInput")
    tri_d = dt("tri", [128, 128], F32, "ExternalInput")
    oh_d = dt("oh", [NB, NB, 128], F32, "ExternalInput")
    XA = dt("scr_xa", [D, S], F32, "Internal")
    XB = dt("scr_xb", [D, S], F32, "Internal")
    HT = dt("scr_h", [D, S], BF16, "Internal")
    YA = dt("scr_ya", [E, S], BF16, "Internal")
    YB = dt("scr_yb", [E, S], BF16, "Internal")

    P = Prog(nc)
    with contextlib.ExitStack() as st:
        cap = 207 * 1024
        arena_h = st.enter_context(nc.sbuf_tensor("arena", [128, cap], U8))
        ar = Arena(arena_h, cap)
        ps = [st.enter_context(nc.psum_tensor("ps%d" % i, [128, 512], F32)) for i in range(8)]
        ps_b = [Buf("ps%d" % i) for i in range(8)]
        const_b = Buf("const")
        env = {'D': D, 'F': F, 'S': S, 'E': E, 'arena': ar, 'ps': ps, 'ps_b': ps_b, 'const_b': const_b, 'rope': rope}
        ones = ar.alloc([128, 128], F32)
        eps6 = ar.alloc([128, 1], F32)
        eps5 = ar.alloc([128, 1], F32)
        ident = ar.alloc([128, 128], BF16)
        tri = ar.alloc([128, 128], BF16)
        oh = ar.alloc([NB, NB, 128], BF16)
        gv = ar.alloc([128, L * 6, KC], F32)
        gvh = ar.alloc([128, L * 2, KC], F32)
        lv = ar.alloc([128, L * 4, 64], F32)
        gs = ar.alloc([128, L, 128], F32)
        pm = ar.alloc([128, NSUB, NB], F32)
        al01 = ar.alloc([128, NSUB, NB], F32)
        own01 = ar.alloc([128, NSUB, NB], F32)
        env.update(ones=ones, eps6=eps6[:, 0:1], eps5=eps5[:, 0:1], ident=ident, tri=tri, oh=oh, pm=pm, al01=al01, own01=own01)
        env['arena_mark'] = ar.off
        P.op('pool', 'memset', [], [const_b], ap=ones[:, :], constant=1.0 / D)
        P.op('pool', 'memset', [], [const_b], ap=eps6[:, :], constant=1e-6)
        P.op('pool', 'memset', [], [const_b], ap=eps5[:, :], constant=1e-5)
        P.dma('pool', ident[:, :], ident_d, [], [const_b])
        P.dma('pool', tri[:, :], tri_d, [], [const_b])
        P.dma('pool', oh[:, :, :], oh_d, [], [const_b])
        P.dma('sp', gv[:, :, :], gv_d, [], [const_b])
        P.dma('sp', lv[:, :, :], lv_d, [], [const_b])
        P.dma('sp', gs[:, :, :], gs_d, [], [const_b])
        P.dma('sp', pm[:, :, :], gconst[0], [], [const_b])
        P.dma('sp', al01[:, :, :], gconst[1], [], [const_b])
        P.dma('sp', own01[:, :, :], gconst[2], [], [const_b])
        for l in range(L):
            for j, src in enumerate((1, 5)):
                P.op('dve', 'tensor_scalar', [const_b], [const_b], out=gvh[:, l * 2 + j, :], in0=gv[:, l * 6 + src, :],
                     scalar1=0.5, scalar2=None, op0=ALU.mult)
        P.barrier()
        all_t = list(range(NT))
        outs = []
        cur = xT
        for l in range(L):
            last = (l == L - 1)
            g = lambda j: gv[:, l * 6 + j, :]
            nxt = XA if cur is not XA else XB
            ffn_phase(P, env, cur, nxt, HT, wgu[0][l], wd[0][l], g(0), gvh[:, l * 2, :], g(2), all_t)
            P.barrier()
            x1 = nxt
            q_tiles = all_t if not last else list(range(own_from, NT))
            attn_phase(P, env, HT, YA, YB, wq[l], lv[:, l * 4:(l + 1) * 4, :], gs[:, l, :], lambda_inits[l], nH, nH, q_tiles)
            P.barrier()
            tl = all_t if not last else list(range(own_from, NT))
            x2 = XA if x1 is not XA else XB
            merge_phase(P, env, x1, HT, YA, YB, x2, wmix[l], wo[l], g(3), tl)
            P.barrier()
            x3 = outT if last else (XA if x2 is not XA else XB)
            o = ffn_phase(P, env, x2, x3, None, wgu[1][l], wd[1][l], g(4), gvh[:, l * 2 + 1, :], None, tl)
            if last:
                outs += o
            P.barrier()
            cur = x3
        fin = P.add('sp', None)
        fin.deps.update(outs)
        P.emit()
    return nc


def prep_ffn_weights(w_gu, F):
    D = w_gu.shape[0]
    KC, FC = D // 128, F // 128
    g = w_gu[:, :F].reshape(KC, 128, FC, 128)
    u = w_gu[:, F:].reshape(KC, 128, FC, 128)
    gu = np.concatenate([g, u], axis=3)
    return np.ascontiguousarray(gu.transpose(2, 1, 0, 3))


def gvec(g):
    return np.ascontiguousarray(g.reshape(-1, 128).T)


def rope_table(S, hd, period, pos0=0):
    theta = np.float32(500000.0)
    rot = hd // 4
    half = rot // 2
    inv = np.power(theta, -np.arange(0, rot, 2, dtype=np.float32) / np.float32(rot)).astype(np.float32)
    ang = ((np.arange(S, dtype=np.float32) + np.float32(pos0))[:, None] * inv[None, :]).astype(np.float32)
    cos, sin = np.cos(ang).astype(np.float32), np.sin(ang).astype(np.float32)
    out = np.zeros((128, 2, S), np.float32)
    out[:, 0, :] = 1.0
    for p in range(128):
        d = p % period
        if d < half:
            out[p, 0] = cos[:, d]
            out[p, 1] = -sin[:, d]
        elif d < rot:
            out[p, 0] = cos[:, d - half]
            out[p, 1] = sin[:, d - half]
    return out


def partner_idx(hd, period):
    rot = hd // 4
    half = rot // 2
    idx = np.arange(128)
    for j in range(128):
        d = j % period
        if d < half:
            idx[j] = j + half
        elif d < rot:
            idx[j] = j - half
    return idx


def attn_consts(S):
    NSUB, NB = S // 128, S // 256
    own = (np.arange(NSUB) // 2)[:, None]
    n = np.arange(NB)[None, :]
    pm = np.where(n < own, 0.0, -1e30).astype(np.float32)
    al = (n < own).astype(np.float32)
    ow = (n == own).astype(np.float32)
    g = np.stack([pm, al, ow])[:, None].repeat(128, axis=1)
    ident = np.eye(128, dtype=np.float32)
    tri = (np.arange(128)[None, :] >= np.arange(128)[:, None]).astype(np.float32)
    oh = np.zeros((NB, NB, 128), np.float32)
    for k in range(NB):
        oh[k, k, :] = 1.0
    rope = np.stack([rope_table(S, 64, 64), rope_table(S, 128, 128)])
    return {"gconst": np.ascontiguousarray(g), "ident": ident, "tri": tri, "oh": oh, "rope": rope}


def prep_attn_w(w_in, D, nH):
    KC = D // 128
    W = nH * 128
    pd, pmo = partner_idx(64, 64), partner_idx(128, 128)
    out = []
    for (oq, ok, ov, pidx) in ((0, W, 2 * W, pd), (3 * W, 4 * W, 5 * W, pmo)):
        for h in range(nH):
            cq = oq + h * 128 + np.arange(128)
            ck = ok + h * 128 + np.arange(128)
            cv = ov + h * 128 + np.arange(128)
            mats = [w_in[:, cq], w_in[:, cq[pidx]], w_in[:, ck], w_in[:, ck[pidx]], w_in[:, cv]]
            out.append(np.stack([m.reshape(KC, 128, 128).transpose(1, 0, 2) for m in mats]))
    return np.ascontiguousarray(np.stack(out)).astype(np.float32)


def prep_merge_w(w_gate, wbd, wbm, w_out):
    D = w_out.shape[0]
    E = wbd.shape[0]
    KC, EC = D // 128, E // 128
    ga = w_gate[:, :D].reshape(KC, 128, KC, 128).transpose(2, 1, 0, 3)
    gb = w_gate[:, D:].reshape(KC, 128, KC, 128).transpose(2, 1, 0, 3)
    bd = wbd.reshape(EC, 128, KC, 128).transpose(2, 1, 0, 3)
    bm = wbm.reshape(EC, 128, KC, 128).transpose(2, 1, 0, 3)
    wmix = np.ascontiguousarray(np.concatenate([ga, gb, bd, bm], axis=2)).astype(np.float32)
    wo = np.ascontiguousarray(w_out.reshape(KC, 128, KC, 128).transpose(2, 1, 0, 3)).astype(np.float32)
    return wmix, wo


def prep_shared_inputs(inp, D, F, S, nH, L):
    f32 = np.float32
    A = lambda k: np.asarray(inp[k], f32)
    W = nH * 128
    sh = dict(attn_consts(S))
    sh["wgu1"] = np.stack([prep_ffn_weights(A('ffn1_w_gu')[l], F) for l in range(L)])
    sh["wgu2"] = np.stack([prep_ffn_weights(A('ffn2_w_gu')[l], F) for l in range(L)])
    sh["wd1"] = np.ascontiguousarray(A('ffn1_w_down'))
    sh["wd2"] = np.ascontiguousarray(A('ffn2_w_down'))
    w_in = A('w_in')
    sh["wq"] = np.stack([prep_attn_w(w_in[l], D, nH) for l in range(L)])
    mw = [prep_merge_w(w_in[l][:, 6 * W:6 * W + 2 * D], A('w_branch_diff')[l], A('w_branch_moba')[l], A('w_out')[l]) for l in range(L)]
    sh["wmix"] = np.stack([m[0] for m in mw])
    sh["wo"] = np.stack([m[1] for m in mw])
    gl = []
    for l in range(L):
        for k in ('ffn1_pre_g', 'ffn1_post_g', 'mix_pre_g', 'mix_post_g', 'ffn2_pre_g', 'ffn2_post_g'):
            gl.append(gvec(A(k)[l]))
    sh["gv"] = np.ascontiguousarray(np.stack(gl, axis=1))
    lv = np.stack([np.stack([A('diff_lq1')[l], A('diff_lq2')[l], A('diff_lk1')[l], A('diff_lk2')[l]]) for l in range(L)])
    sh["lvec"] = np.ascontiguousarray(np.broadcast_to(lv.reshape(1, L * 4, 64), (128, L * 4, 64)))
    sh["gsub"] = np.ascontiguousarray(np.broadcast_to(A('diff_subln_g').reshape(1, L, 128), (128, L, 128)))
    return sh


_D, _F, _B, _S, _L, _NH = 2048, 5632, 4, 4096, 2, 8
_NCORES = 8


def run_fused(inputs, D, F, S, nH, L, B, core_batches):
    f32 = np.float32
    x = np.asarray(inputs['x'], f32)
    sh = prep_shared_inputs(inputs, D, F, S, nH, L)
    lam = [0.8 - 0.6 * math.exp(-0.3 * l) for l in range(L)]
    nc = build_fused(D, F, S, nH, L, lam)
    ims = []
    for b in core_batches:
        im = dict(sh)
        im["xT"] = np.ascontiguousarray(x[b].T)
        ims.append(im)
    res = run_bass_kernel_spmd(nc, ims, core_ids=list(range(len(core_batches)))).results
    out = np.empty((B, S, D), f32)
    done = set()
    for c, b in enumerate(core_batches):
        if b not in done:
            out[b] = np.asarray(res[c]["outT"]).T
            done.add(b)
    return out


def kernel(**inputs):
    return run_fused(inputs, _D, _F, _S, _NH, _L, _B, [c // 2 for c in range(_NCORES)])
```
